# Optimizing a Trainium2 kernel written in Bass

```python
import math
import jax, jax.numpy as jnp
from jax import lax
import numpy as np

D_MODEL = 1024
BATCH = 4
SEQ = 8192
DEPTH = 4

GRID_W = 64
CTX_LEN = 256
N_MIXERS = 2
S5_GROUP = 16
S5_GROUPS = D_MODEL // S5_GROUP
S5_STATE = 64
POOL_WINDOWS = (2, 4, 8, 16)
POOL_GROUPS = len(POOL_WINDOWS)
POOL_CH = D_MODEL // POOL_GROUPS
D_FF = 4 * D_MODEL
N_S5 = (DEPTH + N_MIXERS - 1) // N_MIXERS
N_POOL = DEPTH // N_MIXERS
DT_MIN = 1e-3
DT_MAX = 1e-1
EPS = 1e-6

kernel_name = "hybrid_s5_pool_prefix_dit"


def _rmsnorm(x, g):
    xf = x.astype(jnp.float32)
    y = xf * lax.rsqrt(jnp.mean(xf * xf, axis=-1, keepdims=True) + EPS)
    return (y * g.astype(jnp.float32)).astype(x.dtype)


def _modulate(h, shift, scale):
    return h * (1 + scale) + shift


def _mlp(h, w1, b1, w2, b2):
    a = jax.nn.relu(h @ w1 + b1)
    return (a * a) @ w2 + b2


def _s5_discretize(a_re, a_im, log_dt, b_re, b_im):
    f = jnp.float32
    a_re, a_im, b_re, b_im = a_re.astype(f), a_im.astype(f), b_re.astype(f), b_im.astype(f)
    dt = jnp.exp(log_dt.astype(f))[:, None]
    da_re, da_im = a_re * dt, a_im * dt
    mag = jnp.exp(da_re)
    lb_re, lb_im = mag * jnp.cos(da_im), mag * jnp.sin(da_im)
    den = a_re * a_re + a_im * a_im
    num_re, num_im = lb_re - 1.0, lb_im
    k_re = (num_re * a_re + num_im * a_im) / den
    k_im = (num_im * a_re - num_re * a_im) / den
    bb_re = k_re[..., None] * b_re - k_im[..., None] * b_im
    bb_im = k_re[..., None] * b_im + k_im[..., None] * b_re
    return lb_re, lb_im, da_re, da_im, bb_re, bb_im


def _apply_b(ug, bb_re, bb_im):
    return (jnp.einsum('blgh,gph->blgp', ug, bb_re),
            jnp.einsum('blgh,gph->blgp', ug, bb_im))


def _diag_scan(bu_re, bu_im, da_re, da_im, reverse):
    def combine(a, b):
        n1, r1, i1 = a
        n2, r2, i2 = b
        mag = jnp.exp(n2 * da_re)
        ang = n2 * da_im
        p_re, p_im = mag * jnp.cos(ang), mag * jnp.sin(ang)
        return (n1 + n2, p_re * r1 - p_im * i1 + r2, p_re * i1 + p_im * r1 + i2)
    n = jnp.ones((1, bu_re.shape[1], 1, 1), jnp.float32)
    _, h_re, h_im = lax.associative_scan(combine, (n, bu_re, bu_im), reverse=reverse, axis=1)
    return h_re, h_im


def _readout(h_re, h_im, c_re, c_im):
    f = jnp.float32
    return (jnp.einsum('blgp,ghp->blgh', h_re, c_re.astype(f))
            - jnp.einsum('blgp,ghp->blgh', h_im, c_im.astype(f)))


def _s5_mixer(u, uc, a_re, a_im, log_dt, b_re, b_im, c_re, c_im, d_skip, glu_w, glu_b, ctx_out):
    f = jnp.float32
    bsz, n_lat, _ = u.shape
    n_ctx = uc.shape[1]
    ug = u.astype(f).reshape(bsz, n_lat, S5_GROUPS, S5_GROUP)
    ucg = uc.astype(f).reshape(bsz, n_ctx, S5_GROUPS, S5_GROUP)
    y = jnp.zeros_like(ug)
    yc = jnp.zeros_like(ucg) if ctx_out else None
    for d in range(2):
        rev = d == 1
        lb_re, lb_im, da_re, da_im, bb_re, bb_im = _s5_discretize(
            a_re[d], a_im[d], log_dt[d], b_re[d], b_im[d])
        cu_re, cu_im = _apply_b(ucg, bb_re, bb_im)
        hc_re, hc_im = _diag_scan(cu_re, cu_im, da_re, da_im, rev)
        end = 0 if rev else -1
        h0_re, h0_im = hc_re[:, end], hc_im[:, end]
        if ctx_out:
            yc = yc + _readout(hc_re, hc_im, c_re[d], c_im[d])
        lu_re, lu_im = _apply_b(ug, bb_re, bb_im)
        start = -1 if rev else 0
        lu_re = lu_re.at[:, start].add(lb_re * h0_re - lb_im * h0_im)
        lu_im = lu_im.at[:, start].add(lb_re * h0_im + lb_im * h0_re)
        h_re, h_im = _diag_scan(lu_re, lu_im, da_re, da_im, rev)
        y = y + _readout(h_re, h_im, c_re[d], c_im[d])

    def post(yy, uu):
        out = yy.reshape(uu.shape) + d_skip.astype(f) * uu.astype(f)
        z = jax.nn.gelu(out).astype(uu.dtype)
        return z * jax.nn.sigmoid(z @ glu_w + glu_b)

    return post(y, u), (post(yc, uc) if ctx_out else None)


def _window_bounds(n, w):
    t = jnp.arange(n)
    return jnp.maximum(t - w // 2, 0), jnp.minimum(t + w - w // 2, n)


def _pool_grid(xg, w, rows):
    b, n, ch = xg.shape
    g = xg.reshape(b, rows, GRID_W, ch)
    s = jnp.cumsum(jnp.cumsum(g, axis=1), axis=2)
    s = jnp.pad(s, ((0, 0), (1, 0), (1, 0), (0, 0)))
    r_lo, r_hi = _window_bounds(rows, w)
    c_lo, c_hi = _window_bounds(GRID_W, w)
    s_hi = jnp.take(s, r_hi, axis=1)
    s_lo = jnp.take(s, r_lo, axis=1)
    tot = (jnp.take(s_hi, c_hi, axis=2) - jnp.take(s_hi, c_lo, axis=2)
           - jnp.take(s_lo, c_hi, axis=2) + jnp.take(s_lo, c_lo, axis=2))
    cnt = ((r_hi - r_lo)[:, None] * (c_hi - c_lo)[None, :]).astype(jnp.float32)
    return (tot / cnt[None, :, :, None] - g).reshape(b, n, ch)


def _pool_seq(xs, w):
    n = xs.shape[1]
    s = jnp.pad(jnp.cumsum(xs, axis=1), ((0, 0), (1, 0), (0, 0)))
    lo, hi = _window_bounds(n, w)
    cnt = (hi - lo).astype(jnp.float32)[None, :, None]
    return (jnp.take(s, hi, axis=1) - jnp.take(s, lo, axis=1)) / cnt - xs


def _pool_mixer(u, pool_w, pool_scale, rows):
    uf = u.astype(jnp.float32)
    parts = []
    for gi, w in enumerate(POOL_WINDOWS):
        xg = uf[..., gi * POOL_CH:(gi + 1) * POOL_CH]
        parts.append(_pool_grid(xg, w, rows) if rows is not None else _pool_seq(xg, w))
    p = jnp.stack(parts, axis=2).astype(u.dtype)
    y = jnp.einsum('blgc,gcd->blgd', p, pool_w).reshape(u.shape)
    return y * pool_scale


def setup_inputs(seed: int = 0) -> dict:
    key = jax.random.key(seed)
    ks = jax.random.split(key, 32)
    f = jnp.float32
    G, P, H, D = S5_GROUPS, S5_STATE, S5_GROUP, D_MODEL

    def nrm(k, shape, s):
        return jax.random.normal(k, shape, f) * s

    n_idx = jnp.arange(P, dtype=f)
    return {
        "x": nrm(ks[0], (BATCH, SEQ, D), 1.0),
        "c": nrm(ks[1], (BATCH, D), 1.0),
        "ctx": nrm(ks[2], (BATCH, CTX_LEN, D), 1.0),
        "c_ctx": nrm(ks[3], (D,), 1.0),
        "ada_w": nrm(ks[4], (DEPTH, D, 6 * D), 0.5 * D ** -0.5),
        "ada_b": nrm(ks[5], (DEPTH, 6 * D), 0.02),
        "norm1_g": 1.0 + nrm(ks[6], (DEPTH, D), 0.05),
        "norm2_g": 1.0 + nrm(ks[7], (DEPTH, D), 0.05),
        "s5_a_re": -0.5 + nrm(ks[8], (N_S5, 2, G, P), 0.02),
        "s5_a_im": math.pi * n_idx + nrm(ks[9], (N_S5, 2, G, P), 0.02),
        "s5_log_dt": jax.random.uniform(ks[10], (N_S5, 2, G), f,
                                         math.log(DT_MIN), math.log(DT_MAX)),
        "s5_b_re": nrm(ks[11], (N_S5, 2, G, P, H), H ** -0.5),
        "s5_b_im": nrm(ks[12], (N_S5, 2, G, P, H), H ** -0.5),
        "s5_c_re": nrm(ks[13], (N_S5, 2, G, H, P), P ** -0.5),
        "s5_c_im": nrm(ks[14], (N_S5, 2, G, H, P), P ** -0.5),
        "s5_d": nrm(ks[15], (N_S5, D), 1.0),
        "s5_glu_w": nrm(ks[16], (N_S5, D, D), D ** -0.5),
        "s5_glu_b": nrm(ks[17], (N_S5, D), 0.02),
        "pool_w": nrm(ks[18], (N_POOL, POOL_GROUPS, POOL_CH, POOL_CH), POOL_CH ** -0.5),
        "pool_scale": 1.0 + nrm(ks[19], (N_POOL, D), 0.05),
        "mlp_w1": nrm(ks[20], (DEPTH, D, D_FF), D ** -0.5),
        "mlp_b1": nrm(ks[21], (DEPTH, D_FF), 0.02),
        "mlp_w2": nrm(ks[22], (DEPTH, D_FF, D), D_FF ** -0.5),
        "mlp_b2": nrm(ks[23], (DEPTH, D), 0.02),
        "final_g": 1.0 + nrm(ks[24], (D,), 0.05),
    }


def reference(x, c, ctx, c_ctx, ada_w, ada_b, norm1_g, norm2_g,
              s5_a_re, s5_a_im, s5_log_dt, s5_b_re, s5_b_im, s5_c_re, s5_c_im,
              s5_d, s5_glu_w, s5_glu_b, pool_w, pool_scale,
              mlp_w1, mlp_b1, mlp_w2, mlp_b2, final_g):
    n_tok = x.shape[1]
    rows = n_tok // GRID_W
    last_ctx_reader = ((DEPTH - 1) // N_MIXERS) * N_MIXERS
    silu_c = jax.nn.silu(c)
    silu_cc = jax.nn.silu(c_ctx)
    h_ctx = ctx
    for i in range(DEPTH):
        ctx_in = i <= last_ctx_reader
        ctx_out = i < last_ctx_reader
        j = i // N_MIXERS
        mod = silu_c @ ada_w[i] + ada_b[i]
        sh1, sc1, g1, sh2, sc2, g2 = jnp.split(mod[:, None, :], 6, axis=-1)
        xn = _modulate(_rmsnorm(x, norm1_g[i]), sh1, sc1)
        if ctx_in:
            mod_c = silu_cc @ ada_w[i] + ada_b[i]
            csh1, csc1, cg1, csh2, csc2, cg2 = jnp.split(mod_c, 6)
            cn = _modulate(_rmsnorm(h_ctx, norm1_g[i]), csh1, csc1)
        if i % N_MIXERS == 0:
            y, yc = _s5_mixer(xn, cn, s5_a_re[j], s5_a_im[j], s5_log_dt[j],
                              s5_b_re[j], s5_b_im[j], s5_c_re[j], s5_c_im[j],
                              s5_d[j], s5_glu_w[j], s5_glu_b[j], ctx_out)
        else:
            y = _pool_mixer(xn, pool_w[j], pool_scale[j], rows)
            yc = _pool_mixer(cn, pool_w[j], pool_scale[j], None) if ctx_out else None
        x = x + g1 * y
        x = x + g2 * _mlp(_modulate(_rmsnorm(x, norm2_g[i]), sh2, sc2),
                          mlp_w1[i], mlp_b1[i], mlp_w2[i], mlp_b2[i])
        if ctx_out:
            h_ctx = h_ctx + cg1 * yc
            h_ctx = h_ctx + cg2 * _mlp(_modulate(_rmsnorm(h_ctx, norm2_g[i]), csh2, csc2),
                                       mlp_w1[i], mlp_b1[i], mlp_w2[i], mlp_b2[i])
    return _rmsnorm(x, final_g)
```

```python
import math
import numpy as np
import concourse.bass as bass
import concourse.mybir as mybir
from concourse.bass_utils import run_bass_kernel_spmd

F32 = mybir.dt.float32
BF16 = mybir.dt.bfloat16
AF = mybir.ActivationFunctionType
ALU = mybir.AluOpType

D = 1024
KC = 8
NTOK = 4096
NCTX = 256
NALL = NTOK + NCTX
T = 32
NCH = NALL // T
DFF = 4096
NV = 472
V_N1G, V_N2G, V_ADAB, V_S5D, V_GLUB, V_PSC, V_B2, V_FG, V_B1 = 0, 32, 64, 256, 272, 288, 304, 336, 344
WINS = (2, 4, 8, 16)
SBUF_BYTES = 212864
SB0 = 16512
RG = [[0, 1], [2, 3], [4, 5], [6, 7]]
RG8 = [[0, 1, 2, 3, 4, 5, 6, 7]]
NSH = 8


class Op:
    __slots__ = ("eng", "fn", "reads", "writes", "dma", "idx", "waits", "sig", "sem", "val", "inc")


class _Rec:
    def __getattr__(self, name):
        def f(*a, **kw):
            self.call = (name, a, kw)
            return None
        return f


class Prog:
    ENGS = ("pe", "act", "dve", "pool", "sp")
    NRING = {"sp": 14, "pool": 8, "act": 2, "pe": 2, "dve": 2}

    def __init__(self, nc):
        self.nc = nc
        self.ops = []
        self.esem = {e: nc.alloc_semaphore("es_" + e) for e in self.ENGS}
        self.ring = {e: [nc.alloc_semaphore("dr_%s_%d" % (e, i)) for i in range(n)]
                     for e, n in self.NRING.items()}
        self.regions = {}
        self.byspace = {}

    def region(self, name, space, lo, hi):
        assert name not in self.regions, name
        self.regions[name] = (space, lo, hi)
        self.byspace.setdefault(space, []).append(name)

    def add(self, eng, fn, reads=(), writes=(), dma=False, inc=None, own_sem=False):
        op = Op()
        rec = _Rec()
        fn(rec)
        op.eng, op.fn, op.dma = eng, rec.call, dma
        op.reads, op.writes = tuple(reads), tuple(writes)
        op.inc = inc if inc is not None else (16 if dma else 1)
        op.sig, op.sem, op.val, op.waits = False, None, 0, ()
        if own_sem:
            op.sem = self.nc.alloc_semaphore("own_%d" % len(self.ops))
            op.val = op.inc
            op.sig = True
        op.idx = len(self.ops)
        self.ops.append(op)
        return op

    def dma(self, q, out, in_, reads=(), writes=(), **kw):
        return self.add(q, lambda e: e.dma_start(out=out, in_=in_, **kw), reads, writes, dma=True)

    def finalize(self):
        nc = self.nc
        ov = {}
        for space, names in self.byspace.items():
            for a in names:
                _, lo, hi = self.regions[a]
                ov[a] = [b for b in names if self.regions[b][1] < hi and lo < self.regions[b][2]]
        lastw, readers = {}, {}
        dma_cnt = {e: 0 for e in self.ENGS}
        dma_hist = {e: [] for e in self.ENGS}
        deps_of = []
        for op in self.ops:
            deps = set()
            for r in op.reads:
                for q in ov.get(r, (r,)):
                    w = lastw.get(q)
                    if w is not None:
                        deps.add(w)
            for r in op.writes:
                for q in ov.get(r, (r,)):
                    w = lastw.get(q)
                    if w is not None:
                        deps.add(w)
                    deps.update(readers.get(q, ()))
            deps.discard(op)
            if op.dma:
                n = self.NRING[op.eng]
                i = dma_cnt[op.eng]
                op.sem = self.ring[op.eng][i % n]
                op.val = op.inc * (i // n + 1)
                hist = dma_hist[op.eng]
                if i >= n:
                    deps.add(hist[i - n])
                hist.append(op)
                dma_cnt[op.eng] = i + 1
            for r in op.reads:
                readers.setdefault(r, []).append(op)
            for r in op.writes:
                lastw[r] = op
                readers[r] = []
            deps = [d for d in deps if not (d.eng == "pe" and op.eng == "pe" and not d.dma and not op.dma)]
            deps_of.append(deps)
            for d in deps:
                d.sig = True
        cnt = {e: 0 for e in self.ENGS}
        for op in self.ops:
            if op.dma:
                op.sig = True
                continue
            if op.sig and op.sem is None:
                cnt[op.eng] += 1
                op.sem = self.esem[op.eng]
                op.val = cnt[op.eng]
        waited = {e: {} for e in self.ENGS}
        for op, deps in zip(self.ops, deps_of):
            need = {}
            for d in deps:
                k = id(d.sem)
                if k not in need or need[k][1] < d.val:
                    need[k] = (d.sem, d.val)
            wl = []
            for k, (s, v) in need.items():
                if waited[op.eng].get(k, 0) >= v:
                    continue
                waited[op.eng][k] = v
                wl.append((s, v))
            op.waits = wl
        per = {e: [o for o in self.ops if o.eng == e] for e in self.ENGS}
        tails = {}
        for e in self.ENGS:
            seen = {}
            for o in per[e]:
                if o.dma:
                    seen[id(o.sem)] = (o.sem, o.val)
            tails[e] = [sv for k, sv in seen.items() if waited[e].get(k, 0) < sv[1]]

        def emit(eh, lst, tail):
            for o in lst:
                for (s, v) in o.waits:
                    eh.wait_ge(s, v)
                name, a, kw = o.fn
                ins = getattr(eh, name)(*a, **kw)
                if o.sig:
                    ins.then_inc(o.sem, o.inc)
            for (s, v) in tail:
                eh.wait_ge(s, v)

        with nc.Block() as block:
            @block.tensor
            def _(e):
                emit(e, per["pe"], tails["pe"])

            @block.scalar
            def _(e):
                emit(e, per["act"], tails["act"])

            @block.vector
            def _(e):
                emit(e, per["dve"], tails["dve"])

            @block.gpsimd
            def _(e):
                emit(e, per["pool"], tails["pool"])

            @block.sync
            def _(e):
                emit(e, per["sp"], tails["sp"])
        return {e: len(per[e]) for e in self.ENGS}


def _dtsize(dt):
    return 2 if dt == BF16 else 4


class Builder:
    def __init__(self, dbg=None, nlayers=4):
        self.nc = nc = bass.Bass("TRN2", target_bir_lowering=False)
        self.P = Prog(nc)
        self.dbg = dbg
        self.nlayers = nlayers
        self.uid = 0
        self.build()

    def sb(self, name, shape, dt, off):
        nbytes = int(np.prod(shape[1:])) * _dtsize(dt)
        assert off % 32 == 0 and off + nbytes <= SBUF_BYTES, (name, off, nbytes)
        t = self.nc.alloc_sbuf_tensor_at(name, list(shape), dt, offset=off + SB0)
        self.P.region(name, "sbuf", off, off + nbytes)
        return t

    def sub(self, name, parent_off, lo, hi):
        self.P.region(name, "sbuf", parent_off + lo, parent_off + hi)

    def dram(self, name, shape, dt, kind="Internal"):
        return self.nc.dram_tensor(name, list(shape), dt, kind=kind)

    def pe(self, fn, r, w):
        self.P.add("pe", fn, r, w)

    def act(self, fn, r, w):
        self.P.add("act", fn, r, w)

    def dve(self, fn, r, w):
        self.P.add("dve", fn, r, w)

    def pool(self, fn, r, w):
        self.P.add("pool", fn, r, w)

    def eng(self, which, fn, r, w):
        self.P.add(which, fn, r, w)

    def load(self, out, in_, r, w, **kw):
        self.P.dma("sp", out, in_, r, w, **kw)

    def store(self, out, in_, r, w, **kw):
        self.P.dma("sp", out, in_, r, w, **kw)

    def tp(self, i):
        return (32 * i, 0) if i == 3 else None

    def build(self):
        nc, P = self.nc, self.P
        ap = lambda t: t.ap()
        self.xT = self.dram("xT", [D, NALL], F32, "ExternalInput").ap()
        self.cT = self.dram("cT", [128, KC, 2], F32, "ExternalInput").ap()
        self.vecs_d = self.dram("vecs", [128, NV], F32, "ExternalInput").ap()
        NS = NSH
        self.ada_sh = self.dram("ada_w", [4, D // NS, 6 * D], F32, "ExternalInput").ap()
        self.w1_sh = self.dram("w1", [4, D // NS, DFF], F32, "ExternalInput").ap()
        self.w2_sh = self.dram("w2", [4, DFF // NS, D], F32, "ExternalInput").ap()
        self.glu_sh = self.dram("gluw", [2, D // NS, D], F32, "ExternalInput").ap()
        self.pool_sh = self.dram("poolw", [2, D // NS, 256], F32, "ExternalInput").ap()
        self.gath = []
        def full(name, nl_, rows, cols, sh):
            fulls = []
            for i in range(nl_):
                bt = self.dram("%s_b%d" % (name, i), [rows // NS, cols], F32)
                ft = self.dram("%s_f%d" % (name, i), [rows, cols], F32)
                self.gath.append((name, i, sh[i], bt, ft))
                fulls.append(ft.ap())
            return fulls
        self.ada_w = full("ada", 4, D, 6 * D, self.ada_sh)
        self.w1 = full("w1", 4, D, DFF, self.w1_sh)
        self.w2 = full("w2", 4, DFF, D, self.w2_sh)
        self.gluw = full("glu", 2, D, D, self.glu_sh)
        pw = full("pool", 2, D, 256, self.pool_sh)
        self.poolw = [p.rearrange("(g r) c -> g r c", g=4) for p in pw]
        self.s5p = self.dram("s5p", [2, 128, 3, 2, 32], F32, "ExternalInput").ap()
        self.s5B = self.dram("s5B", [2, 128, 2, 32, 2, 32], F32, "ExternalInput").ap()
        self.s5C = self.dram("s5C", [2, 128, 2, 32, 2, 32], F32, "ExternalInput").ap()
        self.consts_d = self.dram("consts", [128, 168], F32, "ExternalInput").ap()
        self.ptab_d = self.dram("ptab", [128, 4, 4, 64], F32, "ExternalInput").ap()
        self.ctab_d = self.dram("ctab", [128, 4, 2, 256], F32, "ExternalInput").ap()
        self.outT = self.dram("outT", [D, NTOK], F32, "ExternalOutput").ap()
        self.xs = self.dram("xs", [D, NALL], F32).ap()
        self.Z = self.dram("Zs", [D, NALL], F32).ap()
        self.Pq = self.dram("Pq", [D, NALL], BF16).ap()
        self.w1b = self.dram("w1b", [4, D, DFF], BF16).ap()
        self.w2b = self.dram("w2b", [4, DFF, D], BF16).ap()
        self.glub = self.dram("glub", [2, D, D], BF16).ap()
        self.poolb = self.dram("poolb", [2, 4, 256, 256], BF16).ap()
        self.cc_src_t = [self.dram("ccs%d" % i, [128, 64], F32) for i in range(2)]
        self.cc_dst_t = [self.dram("ccd%d" % i, [256, 64], F32) for i in range(2)]
        self.hal_src_t = [self.dram("hls%d" % i, [D, 512], F32) for i in range(2)]
        self.hal_dst_t = [self.dram("hld%d" % i, [2 * D, 512], F32) for i in range(2)]
        if self.dbg:
            self.dbg_out = self.dram("dbg", [D, NALL], F32, "ExternalOutput").ap()

        self.ps = nc.alloc_psum_tensor("ps", [128, 7, 512], F32)
        self.psb = nc.alloc_psum_tensor("psb", [128, 8, 128], BF16)
        for b in range(7):
            P.region(("ps", b), "psum", b * 2048, (b + 1) * 2048)
        P.region("psb", "psum", 7 * 2048, 8 * 2048)
        P.region("dx_all", "dx", 0, NALL)
        for c0, n, _ in self.tiles(True):
            P.region(("dx", c0), "dx", c0, c0 + n)
        P.region("dz_all", "dz", 0, NALL)

        o = 0
        self.vecs = self.sb("vecs_sb", [128, NV], F32, o); o += 2048
        self.consts = self.sb("consts_sb", [128, 168], F32, o); o += 704
        self.identb = self.sb("identb", [128, 128], BF16, o); o += 256
        self.onesb = self.sb("onesb", [128, 128], BF16, o); o += 256
        self.cTs = self.sb("cTs", [128, KC, 2], F32, o); o += 64
        self.modT = self.sb("modT", [128, 4, 48, 2], F32, o); o += 1536
        self.ab = self.sb("ab", [128, 4, 8, KC, 2], F32, o); o += 2048
        self.rstd_o = o
        self.rstd = self.sb("rstd", [128, NALL], F32, o); o += NALL * 4
        self.epsb = self.sb("epsb", [128, 1], F32, o); o += 32
        self.PERS = o
        assert o <= 40000, o

        self.load(self.vecs[:], self.vecs_d[:, :], [], ["vecs_sb"])
        self.load(self.consts[:], self.consts_d[:, :], [], ["consts_sb"])
        self.load(self.cTs[:], self.cT[:, :, :], [], ["cTs"])
        self.dve(lambda e: e.tensor_copy(out=self.identb[:], in_=self.consts[:, 0:128]), ["consts_sb"], ["identb"])
        self.dve(lambda e: e.memset(self.onesb[:], 1.0), [], ["onesb"])
        self.dve(lambda e: e.memset(self.epsb[:], 1e-6), [], ["epsb"])

        self.gather_weights()
        self.prologue_mod()
        self.stats_pass()
        for l in range(self.nlayers):
            if l % 2 == 0:
                self.s5_layer(l)
            else:
                self.pool_layer(l)
            self.phase_c(l)
        if self.dbg:
            src = {"Z": self.Z, "xs": self.xs}[self.dbg]
            for k in range(KC):
                self.load(self.dbg_out[k * 128:(k + 1) * 128, :], src[k * 128:(k + 1) * 128, :], ["dz_all", "dx_all"], [("d", "dbgout", k)])
        self.stats = P.finalize()

    def gather_weights(self):
        P = self.P
        order = {"ada": 0, "glu": 1, "w1": 2, "w2": 3, "pool": 4}
        items = sorted(self.gath, key=lambda t: (t[1] if t[0] != "ada" else -1, order[t[0]]))
        for (name, i, sh, bt, ft) in items:
            self.load(bt.ap()[:, :], sh, [], [("gb", name, i)])
            P.add("pool", lambda e, bt=bt, ft=ft: e.collective_compute("AllGather", ALU.bypass, replica_groups=RG8,
                                                                       ins=[bt.ap().opt()], outs=[ft.ap().opt()]),
                  [("gb", name, i)], [("g", name, i)], inc=1, own_sem=True)

    def layer_casts(self, l):
        P = self.P
        q = "pool"
        if True:
            for r in range(8):
                P.dma(q, self.w1b[l, r * 128:(r + 1) * 128, :], self.w1[l, r * 128:(r + 1) * 128, :], [], [("d", "w1b", l)])
            for r in range(8):
                P.dma(q, self.w2b[l, r * 512:(r + 1) * 512, :], self.w2[l, r * 512:(r + 1) * 512, :], [], [("d", "w2b", l)])
        j = l // 2
        if l % 2 == 0:
            for r in range(2):
                P.dma(q, self.glub[j, r * 512:(r + 1) * 512, :], self.gluw[j, r * 512:(r + 1) * 512, :], [], [("d", "glub", j)])
        else:
            P.dma(q, self.poolb[j].rearrange("g r c -> (g r) c"), self.poolw[j].rearrange("g r c -> (g r) c"), [], [("d", "poolb", j)])

    def prologue_mod(self):
        base = self.PERS
        sc = self.sb("pm_sc", [128, KC, 2], F32, base)
        wp = [self.sb("pm_w%d" % i, [128, KC, 512], F32, base + 64 + i * 16384) for i in range(2)]
        self.act(lambda e: e.activation(out=sc[:], in_=self.cTs[:], func=AF.Silu), ["cTs"], ["pm_sc"])
        it = 0
        for l in range(self.nlayers):
            for pc in range(12):
                w = wp[it % 2]
                wn = "pm_w%d" % (it % 2)
                it += 1
                self.load(w[:], self.ada_w[l][:, pc * 512:(pc + 1) * 512].rearrange("(k p) f -> p k f", p=128), [("g", "ada", l)], [wn])
                for oc in range(4):
                    och = pc * 4 + oc
                    for k in range(KC):
                        self.pe(lambda e, w=w, k=k, oc=oc, och=och: e.matmul(
                            self.ps[:, 6, och * 2:och * 2 + 2], w[:, k, oc * 128:(oc + 1) * 128], sc[:, k, :],
                            start=(k == 0), stop=(k == KC - 1)), [wn, "pm_sc"], [("ps", 6)])
            self.dve(lambda e, l=l: e.tensor_tensor(
                out=self.modT[:, l], in0=self.ps[:, 6, 0:96].rearrange("p (a b) -> p a b", b=2),
                in1=self.vecs[:, V_ADAB + 48 * l:V_ADAB + 48 * (l + 1)].unsqueeze(2).broadcast_to([128, 48, 2]),
                op=ALU.add), [("ps", 6), "vecs_sb"], ["modT"])
            m = self.modT
            ab = self.ab
            for (dst, scc, gcol) in ((0, 8, V_N1G), (2, 32, V_N2G)):
                self.dve(lambda e, l=l, dst=dst, scc=scc, gcol=gcol: e.scalar_tensor_tensor(
                    out=ab[:, l, dst], in0=m[:, l, scc:scc + 8, :], scalar=1.0,
                    in1=self.vecs[:, gcol + 8 * l:gcol + 8 * l + 8].unsqueeze(2).broadcast_to([128, 8, 2]),
                    op0=ALU.add, op1=ALU.mult), ["modT", "vecs_sb"], ["ab"])
            for (dst, src) in ((1, 0), (3, 24), (4, 16), (5, 40)):
                self.dve(lambda e, l=l, dst=dst, src=src: e.tensor_copy(out=ab[:, l, dst], in_=m[:, l, src:src + 8, :]),
                         ["modT"], ["ab"])
            if l % 2 == 1:
                j = l // 2
                self.dve(lambda e, l=l, j=j: e.tensor_tensor(
                    out=ab[:, l, 6], in0=ab[:, l, 4],
                    in1=self.vecs[:, V_PSC + 8 * j:V_PSC + 8 * j + 8].unsqueeze(2).broadcast_to([128, 8, 2]),
                    op=ALU.mult), ["ab", "vecs_sb"], ["ab"])
            self.dve(lambda e, l=l: e.tensor_tensor(
                out=ab[:, l, 7], in0=ab[:, l, 5],
                in1=self.vecs[:, V_B2 + 8 * l:V_B2 + 8 * l + 8].unsqueeze(2).broadcast_to([128, 8, 2]),
                op=ALU.mult), ["ab", "vecs_sb"], ["ab"])

    def rstd_from_ps(self, bank, ncols, col0, tmp, tmpname):
        self.act(lambda e: e.activation(out=tmp[:, 0:ncols], in_=self.ps[:, bank, 0:ncols], func=AF.Sqrt,
                                        scale=1.0 / D, bias=self.epsb[:, 0:1]), [("ps", bank), "epsb"], [tmpname])
        self.dve(lambda e: e.reciprocal(out=self.rstd[:, col0:col0 + ncols], in_=tmp[:, 0:ncols]),
                 [tmpname], [("rstd", col0)])

    def tiles(self, with_ctx):
        t = [(NCTX + i * 512, 512, 0) for i in range(8)]
        if with_ctx:
            t = [(0, NCTX, 1)] + t
        return t

    def stats_pass(self):
        base = self.PERS
        for c0, n, _ in self.tiles(True):
            self.P.region(("rstd", c0), "sbuf", self.rstd_off() + c0 * 4, self.rstd_off() + (c0 + n) * 4)
        xt = [self.sb("sp_x%d" % i, [128, KC, 512], F32, base + i * 16384) for i in range(2)]
        sq = self.sb("sp_sq", [128, KC, 512], BF16, base + 32768)
        tmp = self.sb("sp_tmp", [128, 512], F32, base + 32768 + 8192)
        for it, (c0, n, isctx) in enumerate(self.tiles(True)):
            x = xt[it % 2]
            xn = "sp_x%d" % (it % 2)
            self.load(x[:, :, 0:n], self.xT[:, c0:c0 + n].rearrange("(k p) t -> p k t", p=128), [], [xn])
            self.act(lambda e, x=x, n=n: e.activation(out=sq[:, :, 0:n], in_=x[:, :, 0:n], func=AF.Square), [xn], ["sp_sq"])
            for k in range(KC):
                self.pe(lambda e, k=k, n=n: e.matmul(self.ps[:, 6, 0:n], self.onesb[:], sq[:, k, 0:n],
                                                     start=(k == 0), stop=(k == KC - 1)), ["onesb", "sp_sq"], [("ps", 6)])
            self.rstd_from_ps(6, n, c0, tmp, "sp_tmp")

    def rstd_off(self):
        return self.rstd_o

    def src_x(self, l):
        return self.xT if l == 0 else self.xs

    def xres(self, l, c0):
        return ("d", "x", c0)

    def s5_layer(self, l):
        nc, P = self.nc, self.P
        j = l // 2
        with_ctx_out = (l == 0)
        base = self.PERS
        nm = lambda s: "s5_%d_%s" % (l, s)
        o = base
        S = self.sb(nm("S"), [128, 2, NCH, 2, 32], F32, o); S_off = o; o += 2 * NCH * 64 * 4
        for d in range(2):
            for c in range(NCH):
                P.region((nm("S"), d, c), "sbuf", S_off + (d * NCH + c) * 256, S_off + (d * NCH + c + 1) * 256)
        PR = self.sb(nm("PR"), [128, 33, 2, 32], F32, o); o += 33 * 64 * 4
        PI = self.sb(nm("PI"), [128, 33, 2, 32], F32, o); o += 33 * 64 * 4
        prm = self.sb(nm("prm"), [128, 3, 2, 32], F32, o); o += 768
        sm = self.sb(nm("sm"), [128, 12, 2, 32], F32, o); o += 12 * 256
        ein = self.sb(nm("ein"), [128, 2, 2, 32], F32, o); o += 512
        einsel = self.sb(nm("einsel"), [128, 2, 32], F32, o); o += 256
        stt = self.sb(nm("stt"), [128, 2, 2, 32], F32, o); o += 512
        am1 = self.sb(nm("am1"), [128, 2, 2, 32], F32, o); o += 512
        am2 = self.sb(nm("am2"), [128, 2, 2, 32], F32, o); o += 512
        cb_k = self.sb(nm("cbk"), [128, 2, 2, 4, 2, 32], F32, o); o += 4096
        bbar = self.sb(nm("bbar"), [128, 2, 4, 2, 32], F32, o); o += 2048
        bbarb = self.sb(nm("bbarb"), [128, 2, 4, 2, 32], BF16, o); o += 1024
        tmpa = self.sb(nm("tmpa"), [128, 33, 32], F32, o); o += 4224
        tmpb = self.sb(nm("tmpb"), [128, 33, 32], F32, o); o += 4224
        R0 = o
        VA = self.sb(nm("VA"), [128, 4, 2, 33, 32], BF16, R0)
        WB = self.sb(nm("WB"), [128, 2, 2, 32, 128], BF16, R0 + 16896)
        xbA = self.sb(nm("xbA"), [128, 2048], F32, R0 + 49664)
        ubfA = self.sb(nm("ubfA"), [128, NALL], BF16, R0 + 57856)
        Vd = [self.sb(nm("V%d" % d), [128, 4, 2, 33, 32], BF16, R0 + d * 16896) for d in range(2)]
        Kbd = self.sb(nm("Kbd"), [128, 2, 32, 128], BF16, R0 + 33792)
        Hx = self.sb(nm("Hx"), [128, 2, 2, 4, NCH], BF16, R0 + 50176)
        y32 = self.sb(nm("y32"), [128, 1024], F32, R0 + 54528)
        xbB = self.sb(nm("xbB"), [128, 2048], F32, R0 + 58624)
        ubfB = self.sb(nm("ubfB"), [128, 2048], BF16, R0 + 66816)
        argt = self.sb(nm("argt"), [128, 33, 2, 32], F32, R0)
        argt2 = self.sb(nm("argt2"), [128, 33, 2, 32], F32, R0 + 8448)
        assert R0 + 70912 <= SBUF_BYTES, R0

        self.load(prm[:], self.s5p[j], [], [nm("prm")])
        dt_, dare, daim = sm[:, 0], sm[:, 1], sm[:, 2]
        smn = nm("sm")
        self.act(lambda e: e.activation(out=dt_, in_=prm[:, 2], func=AF.Exp), [nm("prm")], [smn])
        self.dve(lambda e: e.tensor_tensor(out=dare, in0=prm[:, 0], in1=dt_, op=ALU.mult), [nm("prm"), smn], [smn])
        self.dve(lambda e: e.tensor_tensor(out=daim, in0=prm[:, 1], in1=dt_, op=ALU.mult), [nm("prm"), smn], [smn])
        nb = self.consts[:, 132:165].unsqueeze(2).unsqueeze(3).broadcast_to([128, 33, 2, 32])
        self.dve(lambda e: e.tensor_tensor(out=argt[:], in0=nb, in1=dare.unsqueeze(1).broadcast_to([128, 33, 2, 32]),
                                           op=ALU.mult), ["consts_sb", smn], [nm("argt")])
        self.act(lambda e: e.activation(out=argt[:], in_=argt[:], func=AF.Exp), [nm("argt")], [nm("argt")])
        TWO_PI = 2.0 * math.pi
        MAGIC = 12582912.0

        def sincos(dst, dn, shift):
            self.dve(lambda e: e.tensor_tensor(out=argt2[:], in0=nb, in1=daim.unsqueeze(1).broadcast_to([128, 33, 2, 32]),
                                               op=ALU.mult), ["consts_sb", smn], [nm("argt2")])
            if shift != 0.0:
                self.dve(lambda e: e.tensor_scalar(out=argt2[:], in0=argt2[:], scalar1=shift, scalar2=None, op0=ALU.add),
                         [nm("argt2")], [nm("argt2")])
            self.dve(lambda e: e.tensor_scalar(out=dst[:], in0=argt2[:], scalar1=1.0 / TWO_PI, scalar2=MAGIC,
                                               op0=ALU.mult, op1=ALU.add), [nm("argt2")], [dn])
            self.dve(lambda e: e.tensor_scalar(out=dst[:], in0=dst[:], scalar1=-MAGIC, scalar2=None, op0=ALU.add), [dn], [dn])
            self.dve(lambda e: e.scalar_tensor_tensor(out=argt2[:].rearrange("p a b c -> p (a b c)"),
                                                      in0=dst[:].rearrange("p a b c -> p (a b c)"), scalar=-TWO_PI,
                                                      in1=argt2[:].rearrange("p a b c -> p (a b c)"),
                                                      op0=ALU.mult, op1=ALU.add), [dn, nm("argt2")], [nm("argt2")])
            self.dve(lambda e: e.tensor_scalar(out=argt2[:], in0=argt2[:], scalar1=math.pi, scalar2=-math.pi,
                                               op0=ALU.min, op1=ALU.max), [nm("argt2")], [nm("argt2")])
            self.act(lambda e: e.activation(out=dst[:], in_=argt2[:], func=AF.Sin), [nm("argt2")], [dn])
            self.dve(lambda e: e.tensor_tensor(out=dst[:], in0=dst[:], in1=argt[:], op=ALU.mult), [dn, nm("argt")], [dn])

        sincos(PI, nm("PI"), 0.0)
        sincos(PR, nm("PR"), math.pi / 2)
        num_re, den, kre, kim, t7 = sm[:, 3], sm[:, 4], sm[:, 5], sm[:, 6], sm[:, 7]
        are, aim = prm[:, 0], prm[:, 1]
        r_ = [nm("PR"), nm("PI"), nm("prm"), smn]
        tt = lambda o_, a, b, op: self.dve(lambda e: e.tensor_tensor(out=o_, in0=a, in1=b, op=op), r_, [smn])
        self.dve(lambda e: e.tensor_scalar(out=num_re, in0=PR[:, 1], scalar1=-1.0, scalar2=None, op0=ALU.add), r_, [smn])
        tt(den, are, are, ALU.mult)
        tt(t7, aim, aim, ALU.mult)
        tt(den, den, t7, ALU.add)
        self.dve(lambda e: e.reciprocal(out=den, in_=den), r_, [smn])
        tt(kre, num_re, are, ALU.mult)
        tt(t7, PI[:, 1], aim, ALU.mult)
        tt(kre, kre, t7, ALU.add)
        tt(kre, kre, den, ALU.mult)
        tt(kim, PI[:, 1], are, ALU.mult)
        tt(t7, num_re, aim, ALU.mult)
        tt(kim, kim, t7, ALU.subtract)
        tt(kim, kim, den, ALU.mult)
        for d in range(2):
            for h in range(2):
                self.dve(lambda e, d=d, h=h: e.tensor_copy(out=am1[:, d, h], in_=PR[:, 32, d]), r_, [nm("am1")])
            self.dve(lambda e, d=d: e.tensor_scalar(out=am2[:, d, 0], in0=PI[:, 32, d], scalar1=-1.0, scalar2=None,
                                                    op0=ALU.mult), r_, [nm("am2")])
            self.dve(lambda e, d=d: e.tensor_copy(out=am2[:, d, 1], in_=PI[:, 32, d]), r_, [nm("am2")])

        A1c = lambda k, col: self.ab[:, l, 0, k, col:col + 1]
        B1c = lambda k, col: self.ab[:, l, 1, k, col:col + 1]
        src = self.src_x(l)
        blocks = [(0, NCTX, 1), (NCTX, 2048, 0), (NCTX + 2048, 2048, 0)]

        def load_u(k, c0, n, col, x, xn_, ub, ubn, uoff):
            self.load(x[:, 0:n], src[k * 128:(k + 1) * 128, c0:c0 + n], ["dx_all"], [xn_])
            self.dve(lambda e: e.tensor_tensor(out=x[:, 0:n], in0=x[:, 0:n], in1=self.rstd[:, c0:c0 + n], op=ALU.mult),
                     [xn_, "rstd"], [xn_])
            self.act(lambda e: e.activation(out=x[:, 0:n], in_=x[:, 0:n], func=AF.Identity, scale=A1c(k, col), bias=B1c(k, col)),
                     [xn_, "ab"], [xn_])
            self.dve(lambda e: e.tensor_copy(out=ub[:, uoff:uoff + n], in_=x[:, 0:n]), [xn_], [ubn])

        def cplx_table(V, vn, k4, src_re, src_im, ntab, d, negim, srcnames):
            for i in range(4):
                gp = 4 * k4 + i
                sre = src_re(i).unsqueeze(1).broadcast_to([128, ntab, 32])
                sim = src_im(i).unsqueeze(1).broadcast_to([128, ntab, 32])
                pr = PR[:, 0:ntab, d, gp].unsqueeze(2).broadcast_to([128, ntab, 32])
                pi = PI[:, 0:ntab, d, gp].unsqueeze(2).broadcast_to([128, ntab, 32])
                ta, tb = tmpa[:, 0:ntab, :], tmpb[:, 0:ntab, :]
                rr = [nm("PR"), nm("PI"), nm("tmpa"), nm("tmpb")] + srcnames
                pl = lambda fn, w: self.dve(fn, rr, w)
                pl(lambda e, sre=sre, pr=pr, ta=ta: e.tensor_tensor(out=ta, in0=sre, in1=pr, op=ALU.mult), [nm("tmpa")])
                pl(lambda e, sim=sim, pi=pi, tb=tb: e.tensor_tensor(out=tb, in0=sim, in1=pi, op=ALU.mult), [nm("tmpb")])
                pl(lambda e, i=i, ta=ta, tb=tb: e.tensor_tensor(out=V[:, i, 0, 0:ntab, :], in0=ta, in1=tb, op=ALU.subtract), [vn])
                pl(lambda e, sre=sre, pi=pi, ta=ta: e.tensor_tensor(out=ta, in0=sre, in1=pi, op=ALU.mult), [nm("tmpa")])
                pl(lambda e, sim=sim, pr=pr, tb=tb: e.tensor_tensor(out=tb, in0=sim, in1=pr, op=ALU.mult), [nm("tmpb")])
                if negim:
                    pl(lambda e, ta=ta, tb=tb: e.tensor_tensor(out=ta, in0=ta, in1=tb, op=ALU.add), [nm("tmpa")])
                    pl(lambda e, i=i, ta=ta: e.tensor_scalar(out=V[:, i, 1, 0:ntab, :], in0=ta, scalar1=-1.0, scalar2=None,
                                                            op0=ALU.mult), [vn])
                else:
                    pl(lambda e, i=i, ta=ta, tb=tb: e.tensor_tensor(out=V[:, i, 1, 0:ntab, :], in0=ta, in1=tb, op=ALU.add), [vn])

        def load_params(k):
            self.load(cb_k[:, 0], self.s5B[j, :, :, 4 * k:4 * k + 4, :, :], [], [nm("cbk")])
            self.load(cb_k[:, 1], self.s5C[j, :, :, 4 * k:4 * k + 4, :, :], [], [nm("cbk")])
            for d in range(2):
                kr = kre[:, d, 4 * k:4 * k + 4].unsqueeze(2).broadcast_to([128, 4, 32])
                ki = kim[:, d, 4 * k:4 * k + 4].unsqueeze(2).broadcast_to([128, 4, 32])
                bre, bim = cb_k[:, 0, d, :, 0, :], cb_k[:, 0, d, :, 1, :]
                ta = tmpa[:, 0:4, :]
                tb = tmpb[:, 0:4, :]
                rr = [smn, nm("cbk"), nm("tmpa"), nm("tmpb")]
                pl = lambda fn, w: self.dve(fn, rr, w)
                pl(lambda e, bre=bre, kr=kr, ta=ta: e.tensor_tensor(out=ta, in0=bre, in1=kr, op=ALU.mult), [nm("tmpa")])
                pl(lambda e, bim=bim, ki=ki, tb=tb: e.tensor_tensor(out=tb, in0=bim, in1=ki, op=ALU.mult), [nm("tmpb")])
                pl(lambda e, d=d, ta=ta, tb=tb: e.tensor_tensor(out=bbar[:, d, :, 0, :], in0=ta, in1=tb, op=ALU.subtract), [nm("bbar")])
                pl(lambda e, bre=bre, ki=ki, ta=ta: e.tensor_tensor(out=ta, in0=bre, in1=ki, op=ALU.mult), [nm("tmpa")])
                pl(lambda e, bim=bim, kr=kr, tb=tb: e.tensor_tensor(out=tb, in0=bim, in1=kr, op=ALU.mult), [nm("tmpb")])
                pl(lambda e, d=d, ta=ta, tb=tb: e.tensor_tensor(out=bbar[:, d, :, 1, :], in0=ta, in1=tb, op=ALU.add), [nm("bbar")])
            self.dve(lambda e: e.tensor_copy(out=bbarb[:], in_=bbar[:]), [nm("bbar")], [nm("bbarb")])

        for k in range(KC):
            load_params(k)
            for (c0, n, col) in blocks:
                load_u(k, c0, n, col, xbA, nm("xbA"), ubfA, nm("ubfA"), c0)
            for d in range(2):
                cplx_table(VA, nm("VA"), k, lambda i, d=d: bbar[:, d, i, 0, :], lambda i, d=d: bbar[:, d, i, 1, :], 32, d, False, [nm("bbar")])
                for reim in range(2):
                    for m0 in range(0, 32, 8):
                        for ms in range(8):
                            for i in range(4):
                                self.pe(lambda e, i=i, reim=reim, m=m0 + ms, ms=ms: e.transpose(
                                    self.psb[32 * i:32 * i + 32, ms, :], VA[:, i, reim, m, :], self.identb[:],
                                    tile_position=(0, 32 * i)), [nm("VA"), "identb"], ["psb"])
                        self.act(lambda e, d=d, reim=reim, m0=m0: e.activation(
                            out=WB[:, d, reim, m0:m0 + 8, :], in_=self.psb[:], func=AF.Copy), ["psb"], [nm("WB")])
            acc = 0
            for i in range(4):
                for d in range(2):
                    for reim in range(2):
                        a_ = d * 2 + reim
                        bank, slot = (i % 2) * 2 + a_ // 2, a_ % 2
                        acc += 1
                        dst = self.ps[:, bank, slot * NCH:(slot + 1) * NCH]
                        for m in range(32):
                            jj = (31 - m) if d == 0 else m
                            rhs = ubfA[32 * i:32 * i + 32, :].rearrange("p (c t) -> p c t", t=T)[:, :, jj]
                            self.pe(lambda e, i=i, d=d, reim=reim, m=m, dst=dst, rhs=rhs: e.matmul(
                                dst, WB[32 * i:32 * i + 32, d, reim, m, :], rhs, start=(m == 0), stop=(m == 31),
                                tile_position=self.tp(i)), [nm("WB"), nm("ubfA")], [("ps", bank)])
                        gp = 4 * k + i
                        if acc % 2 == 0:
                            self.act(lambda e, d=d, reim=reim, gp=gp, dst=dst: e.activation(
                                out=S[:, d, :, reim, gp], in_=dst, func=AF.Copy), [("ps", bank)], [nm("S")])
                        else:
                            self.dve(lambda e, d=d, reim=reim, gp=gp, dst=dst: e.tensor_copy(
                                out=S[:, d, :, reim, gp], in_=dst), [("ps", bank)], [nm("S")])

        def scan_step(d, c, prev_ap, prev_name):
            rr = [prev_name, nm("am1"), nm("am2"), nm("stt")]
            self.dve(lambda e: e.tensor_tensor(out=stt[:, 0], in0=prev_ap, in1=am1[:, d], op=ALU.mult), rr, [nm("stt")])
            self.dve(lambda e: e.tensor_tensor(out=stt[:, 1], in0=prev_ap[:, ::-1, :], in1=am2[:, d], op=ALU.mult), rr, [nm("stt")])
            self.dve(lambda e: e.tensor_tensor(out=stt[:, 0], in0=stt[:, 0], in1=stt[:, 1], op=ALU.add), rr, [nm("stt")])
            self.dve(lambda e: e.tensor_tensor(out=S[:, d, c], in0=S[:, d, c], in1=stt[:, 0], op=ALU.add),
                     [nm("stt"), (nm("S"), d, c)], [(nm("S"), d, c)])

        for c in range(1, NCH):
            scan_step(0, c, S[:, 0, c - 1], (nm("S"), 0, c - 1))
        si = l // 2
        srcT, dstT = self.cc_src_t[si], self.cc_dst_t[si]
        self.store(srcT.ap()[:, :], S[:, 0, NCH - 1].rearrange("p a b -> p (a b)"), [(nm("S"), 0, NCH - 1)], [("d", "ccs", si)])
        P.add("pool", lambda e: e.collective_compute("AllGather", ALU.bypass, replica_groups=RG,
                                                     ins=[srcT.ap().opt()], outs=[dstT.ap().opt()]),
              [("d", "ccs", si)], [("d", "ccd", si)], inc=1, own_sem=True)
        self.load(ein[:].rearrange("p s a b -> p s (a b)"), dstT.ap().rearrange("(s p) f -> p s f", s=2), [("d", "ccd", si)], [nm("ein")])
        self.dve(lambda e: e.tensor_scalar(out=einsel[:], in0=ein[:, 1], scalar1=self.consts[:, 128:129], scalar2=None, op0=ALU.mult),
                 [nm("ein"), "consts_sb"], [nm("einsel")])
        self.dve(lambda e: e.scalar_tensor_tensor(out=einsel[:], in0=ein[:, 0], scalar=self.consts[:, 129:130], in1=einsel[:],
                                                  op0=ALU.mult, op1=ALU.add), [nm("ein"), "consts_sb", nm("einsel")], [nm("einsel")])
        scan_step(1, NCH - 1, einsel[:], nm("einsel"))
        for c in range(NCH - 2, 7, -1):
            scan_step(1, c, S[:, 1, c + 1], (nm("S"), 1, c + 1))
        for c in range(6, -1, -1):
            scan_step(1, c, S[:, 1, c + 1], (nm("S"), 1, c + 1))

        self.dve(lambda e: e.memset(Kbd[:], 0.0), [], [nm("Kbd")])
        blist = blocks if with_ctx_out else blocks[1:]
        for k in range(KC):
            load_params(k)
            sk = lambda d, lo, hi: S[:, d, lo:hi, :, 4 * k:4 * k + 4].rearrange("p c r g -> p r g c")
            self.dve(lambda e: e.memset(Hx[:, 0, :, :, 0:1], 0.0), [], [nm("Hx")])
            self.dve(lambda e: e.memset(Hx[:, 1, :, :, 7:8], 0.0), [], [nm("Hx")])
            for r in range(2):
                self.dve(lambda e, r=r, sk=sk: e.tensor_copy(out=Hx[:, 0, r, :, 1:NCH], in_=sk(0, 0, NCH - 1)[:, r]), [nm("S")], [nm("Hx")])
                self.dve(lambda e, r=r, sk=sk: e.tensor_copy(out=Hx[:, 1, r, :, 0:7], in_=sk(1, 1, 8)[:, r]), [nm("S")], [nm("Hx")])
                self.dve(lambda e, r=r, sk=sk: e.tensor_copy(out=Hx[:, 1, r, :, 8:NCH - 1], in_=sk(1, 9, NCH)[:, r]), [nm("S")], [nm("Hx")])
            self.dve(lambda e, k=k: e.tensor_copy(out=Hx[:, 1, :, :, NCH - 1], in_=einsel[:, :, 4 * k:4 * k + 4]), [nm("einsel")], [nm("Hx")])
            for d in range(2):
                cplx_table(Vd[d], nm("V%d" % d), k, lambda i, d=d: cb_k[:, 1, d, i, 0, :], lambda i, d=d: cb_k[:, 1, d, i, 1, :],
                           33, d, True, [nm("cbk")])
                for t0 in range(0, 32, 16):
                    for ts in range(16):
                        tau = t0 + ts
                        for i in range(4):
                            for reim in range(2):
                                self.pe(lambda e, i=i, d=d, reim=reim, tau=tau, ts=ts: e.matmul(
                                    self.ps[32 * i:32 * i + 32, 5, ts * 32:(ts + 1) * 32],
                                    bbarb[:, d, i, reim, :], Vd[d][:, i, reim, tau, :],
                                    start=(reim == 0), stop=(reim == 1), tile_position=(0, 32 * i)),
                                    [nm("bbarb"), nm("V%d" % d)], [("ps", 5)])
                    for i in range(4):
                        self.act(lambda e, i=i, d=d, t0=t0: e.activation(
                            out=Kbd[32 * i:32 * i + 32, d, t0:t0 + 16, 32 * i:32 * i + 32],
                            in_=self.ps[32 * i:32 * i + 32, 5, :].rearrange("p (t c) -> p t c", c=32), func=AF.Copy),
                            [("ps", 5)], [nm("Kbd")])
            for (c0, n, col) in blist:
                nchk = n // T
                ch0 = c0 // T
                nbank = max(1, (32 * nchk) // 512)
                jb = 32 // nbank
                load_u(k, c0, n, col, xbB, nm("xbB"), ubfB, nm("ubfB"), 0)
                uv = ubfB[:, 0:n].rearrange("p (c t) -> p t c", t=T)

                def pview(b, rows=None):
                    rs = slice(None) if rows is None else slice(32 * rows, 32 * rows + 32)
                    if nbank > 1:
                        return self.ps[rs, b, :].rearrange("p (j c) -> p j c", c=nchk)
                    return self.ps[rs, 0, 0:32 * nchk].rearrange("p (j c) -> p j c", c=nchk)

                for d in range(2):
                    for tau in range(32):
                        for b in range(nbank):
                            if d == 0:
                                jlo, jhi, sh = max(b * jb, tau), (b + 1) * jb, -tau
                            else:
                                jlo, jhi, sh = b * jb, min((b + 1) * jb, 32 - tau), tau
                            if jlo >= jhi:
                                continue
                            out = pview(b)[:, jlo - b * jb:jhi - b * jb, :]
                            rhs = uv[:, jlo + sh:jhi + sh, :]
                            st = (d == 0 and tau == 0)
                            self.pe(lambda e, d=d, tau=tau, out=out, rhs=rhs, st=st: e.matmul(
                                out, Kbd[:, d, tau, :], rhs, start=st, stop=False, skip_group_check=True),
                                [nm("Kbd"), nm("ubfB")], [("ps", b)])
                    for jj in range(32):
                        n_pow = (jj + 1) if d == 0 else (32 - jj)
                        b = jj // jb
                        for i in range(4):
                            for reim in range(2):
                                out = pview(b, i)[:, jj - b * jb, :]
                                last = (d == 1 and jj == 31 and i == 3 and reim == 1)
                                self.pe(lambda e, i=i, d=d, reim=reim, n_pow=n_pow, out=out, last=last: e.matmul(
                                    out, Vd[d][:, i, reim, n_pow, :], Hx[:, d, reim, i, ch0:ch0 + nchk],
                                    start=False, stop=last, skip_group_check=True, tile_position=(0, 32 * i)),
                                    [nm("V%d" % d), nm("Hx")], [("ps", b)])
                banks = [("ps", b) for b in range(nbank)]
                dsk = self.vecs[:, V_S5D + 8 * j + k:V_S5D + 8 * j + k + 1]
                self.dve(lambda e, n=n, dsk=dsk: e.tensor_scalar(out=xbB[:, 0:n], in0=xbB[:, 0:n], scalar1=dsk, scalar2=None, op0=ALU.mult),
                         [nm("xbB"), "vecs_sb"], [nm("xbB")])
                nh = max(1, n // 1024)
                hn = n // nh
                hc = hn // T
                for h in range(nh):
                    if nbank > 1:
                        psv = self.ps[:, 0:nbank, :].rearrange("p b (j c) -> p c b j", c=nchk)[:, h * hc:(h + 1) * hc]
                        yv = y32[:, 0:hn].rearrange("p (c b j) -> p c b j", b=nbank, j=jb)
                        xv = xbB[:, h * hn:(h + 1) * hn].rearrange("p (c b j) -> p c b j", b=nbank, j=jb)
                    else:
                        psv = self.ps[:, 0, 0:n].rearrange("p (j c) -> p c j", c=nchk)
                        yv = y32[:, 0:hn].rearrange("p (c j) -> p c j", j=32)
                        xv = xbB[:, 0:hn].rearrange("p (c j) -> p c j", j=32)
                    self.dve(lambda e, xv=xv, psv=psv, yv=yv: e.tensor_tensor(out=yv, in0=xv, in1=psv, op=ALU.add),
                             banks + [nm("xbB")], [nm("y32")])
                    self.act(lambda e, hn=hn: e.activation(out=y32[:, 0:hn], in_=y32[:, 0:hn], func=AF.Gelu), [nm("y32")], [nm("y32")])
                    self.store(self.Z[k * 128:(k + 1) * 128, c0 + h * hn:c0 + (h + 1) * hn], y32[:, 0:hn], [nm("y32")], ["dz_all"])

    def pool_layer(self, l):
        nc, P = self.nc, self.P
        j = l // 2
        with_ctx = (l == 1)
        base = self.PERS
        nm = lambda s: "pl_%d_%s" % (l, s)
        o = base
        Pb = [self.sb(nm("P%d" % i), [128, 80, 80], F32, o + i * 25600) for i in range(2)]
        o += 51200
        Yb = [self.sb(nm("Y%d" % i), [128, 80, 64], F32, o + i * 20480) for i in range(2)]
        o += 40960
        xb = self.sb(nm("xb"), [128, NALL], F32, o); o += NALL * 4
        pk = self.sb(nm("pk"), [128, NALL], BF16, o); o += NALL * 2
        hs = self.sb(nm("hs"), [128, 2, 512], F32, o); o += 4096
        hsel = self.sb(nm("hsel"), [128, 512], F32, o); o += 2048
        cbf = [self.sb(nm("cb%d" % i), [128, 272], F32, o + i * 1088) for i in range(2)]
        o += 2176
        hx = self.sb(nm("hx"), [128, KC, 512], F32, o); o += 16384
        self.ptab = self.sb(nm("ptab"), [128, 4, 4, 64], F32, o); o += 4096
        self.ctab = self.sb(nm("ctab"), [128, 4, 2, 256], F32, o); o += 8192
        assert o <= SBUF_BYTES, o
        self.load(self.ptab[:], self.ptab_d[:, :, :, :], [], [nm("ptab")])
        self.load(self.ctab[:], self.ctab_d[:, :, :, :], [], [nm("ctab")])
        src = self.src_x(l)
        A1c = lambda k, col: self.ab[:, l, 0, k, col:col + 1]
        B1c = lambda k, col: self.ab[:, l, 1, k, col:col + 1]
        si = l // 2
        c0 = NCTX + NTOK - 512
        self.load(hx[:], src[:, c0:c0 + 512].rearrange("(k p) t -> p k t", p=128), ["dx_all"], [nm("hx")])
        for k in range(KC):
            self.dve(lambda e, k=k: e.tensor_tensor(out=hx[:, k], in0=hx[:, k], in1=self.rstd[:, c0:c0 + 512], op=ALU.mult),
                     [nm("hx"), "rstd"], [nm("hx")])
            self.act(lambda e, k=k: e.activation(out=hx[:, k], in_=hx[:, k], func=AF.Identity, scale=A1c(k, 0), bias=B1c(k, 0)),
                     [nm("hx"), "ab"], [nm("hx")])
        srcT, dstT = self.hal_src_t[si], self.hal_dst_t[si]
        self.store(srcT.ap().rearrange("(k p) t -> p k t", p=128), hx[:], [nm("hx")], [("d", "hls", si)])
        P.add("pool", lambda e: e.collective_compute("AllGather", ALU.bypass, replica_groups=RG,
                                                     ins=[srcT.ap().opt()], outs=[dstT.ap().opt()]),
              [("d", "hls", si)], [("d", "hld", si)], inc=1, own_sem=True)
        for i in range(2):
            self.dve(lambda e, i=i: e.memset(Pb[i][:], 0.0), [], [nm("P%d" % i)])
            self.dve(lambda e, i=i: e.memset(cbf[i][:], 0.0), [], [nm("cb%d" % i)])
        for k in range(KC):
            wi = k // 2
            w = WINS[wi]
            half = w // 2
            nst = int(math.log2(w))
            en = "dve" if k % 2 == 0 else "pool"
            self.load(xb[:], src[k * 128:(k + 1) * 128, :], ["dx_all"], [nm("xb")])
            self.dve(lambda e: e.tensor_tensor(out=xb[:], in0=xb[:], in1=self.rstd[:], op=ALU.mult), [nm("xb"), "rstd"], [nm("xb")])
            self.act(lambda e, k=k: e.activation(out=xb[:, 0:NCTX], in_=xb[:, 0:NCTX], func=AF.Identity, scale=A1c(k, 1), bias=B1c(k, 1)),
                     [nm("xb"), "ab"], [nm("xb")])
            self.act(lambda e, k=k: e.activation(out=xb[:, NCTX:], in_=xb[:, NCTX:], func=AF.Identity, scale=A1c(k, 0), bias=B1c(k, 0)),
                     [nm("xb"), "ab"], [nm("xb")])
            self.load(hs[:], dstT.ap()[:, :].rearrange("(s k p) t -> p s k t", s=2, p=128)[:, :, k, :], [("d", "hld", si)], [nm("hs")])
            self.dve(lambda e: e.tensor_scalar(out=hsel[:], in0=hs[:, 1], scalar1=self.consts[:, 128:129], scalar2=None, op0=ALU.mult),
                     [nm("hs"), "consts_sb"], [nm("hsel")])
            self.dve(lambda e: e.scalar_tensor_tensor(out=hsel[:], in0=hs[:, 0], scalar=self.consts[:, 129:130], in1=hsel[:],
                                                      op0=ALU.mult, op1=ALU.add), [nm("hs"), "consts_sb", nm("hsel")], [nm("hsel")])
            self.eng(en, lambda e: e.tensor_copy(out=Pb[0][:, 8:72, 8:72], in_=xb[:, NCTX:].rearrange("p (r c) -> p r c", c=64)),
                     [nm("xb")], [nm("P0")])
            self.eng(en, lambda e: e.tensor_copy(out=Pb[0][:, 72:80, 8:72], in_=hsel[:].rearrange("p (r c) -> p r c", c=64)[:, ::-1, ::-1]),
                     [nm("hsel")], [nm("P0")])
            cur = 0
            s = 1
            for _ in range(nst):
                a, b = Pb[cur], Pb[1 - cur]
                self.eng(en, lambda e, a=a, b=b, s=s: e.tensor_tensor(out=b[:, :, s:80], in0=a[:, :, s:80], in1=a[:, :, 0:80 - s], op=ALU.add),
                         [nm("P%d" % cur)], [nm("P%d" % (1 - cur))])
                cur = 1 - cur
                s *= 2
            Xc = Pb[cur]
            TAc = self.ptab[:, wi, 0, :].unsqueeze(1).broadcast_to([128, 80, 64])
            TBc = self.ptab[:, wi, 1, :].unsqueeze(1).broadcast_to([128, 80, 64])
            lo = 8 + half - 1
            self.eng(en, lambda e, Xc=Xc, lo=lo: e.tensor_tensor(out=Yb[0][:], in0=Xc[:, :, lo:lo + 64], in1=TAc, op=ALU.mult),
                     [nm("P%d" % cur), nm("ptab")], [nm("Y0")])
            self.eng(en, lambda e, Xc=Xc, lo=lo: e.tensor_tensor(out=Yb[1][:], in0=Xc[:, :, lo + 1:lo + 65], in1=TBc, op=ALU.mult),
                     [nm("P%d" % cur), nm("ptab")], [nm("Y1")])
            self.eng(en, lambda e: e.tensor_tensor(out=Yb[0][:], in0=Yb[0][:], in1=Yb[1][:], op=ALU.add), [nm("Y0"), nm("Y1")], [nm("Y0")])
            ycur = 0
            s = 1
            for _ in range(nst):
                a, b = Yb[ycur], Yb[1 - ycur]
                self.eng(en, lambda e, a=a, b=b, s=s: e.tensor_tensor(out=b[:, s:80, :], in0=a[:, s:80, :], in1=a[:, 0:80 - s, :], op=ALU.add),
                         [nm("Y%d" % ycur)], [nm("Y%d" % (1 - ycur))])
                ycur = 1 - ycur
                s *= 2
            Yc = Yb[ycur]
            Yo = Yb[1 - ycur]
            TAr = self.ptab[:, wi, 2, :].unsqueeze(2).broadcast_to([128, 64, 64])
            TBr = self.ptab[:, wi, 3, :].unsqueeze(2).broadcast_to([128, 64, 64])
            t1 = Pb[0][:, 0:64, 0:64]
            t2 = Pb[1][:, 0:64, 0:64]
            self.eng(en, lambda e, Yc=Yc, lo=lo: e.tensor_tensor(out=t1, in0=Yc[:, lo:lo + 64, :], in1=TAr, op=ALU.mult),
                     [nm("Y%d" % ycur), nm("ptab")], [nm("P0")])
            self.eng(en, lambda e, Yc=Yc, lo=lo: e.tensor_tensor(out=t2, in0=Yc[:, lo + 1:lo + 65, :], in1=TBr, op=ALU.mult),
                     [nm("Y%d" % ycur), nm("ptab")], [nm("P1")])
            self.eng(en, lambda e: e.tensor_tensor(out=t1, in0=t1, in1=t2, op=ALU.add), [nm("P0"), nm("P1")], [nm("P0")])
            self.eng(en, lambda e: e.tensor_tensor(out=pk[:, NCTX:].rearrange("p (r c) -> p r c", c=64), in0=t1,
                                                   in1=xb[:, NCTX:].rearrange("p (r c) -> p r c", c=64), op=ALU.subtract),
                     [nm("P0"), nm("xb")], [nm("pk")])
            self.eng(en, lambda e: e.memset(Pb[0][:], 0.0), [], [nm("P0")])
            self.eng(en, lambda e: e.memset(Pb[1][:], 0.0), [], [nm("P1")])
            if with_ctx:
                self.eng(en, lambda e: e.tensor_copy(out=cbf[0][:, 8:264], in_=xb[:, 0:NCTX]), [nm("xb")], [nm("cb0")])
                cc = 0
                s = 1
                for _ in range(nst):
                    a, b = cbf[cc], cbf[1 - cc]
                    self.eng(en, lambda e, a=a, b=b, s=s: e.tensor_tensor(out=b[:, s:272], in0=a[:, s:272], in1=a[:, 0:272 - s], op=ALU.add),
                             [nm("cb%d" % cc)], [nm("cb%d" % (1 - cc))])
                    cc = 1 - cc
                    s *= 2
                Xc = cbf[cc]
                Xo = cbf[1 - cc]
                self.eng(en, lambda e, Xc=Xc, Xo=Xo, lo=lo: e.tensor_tensor(out=Xo[:, 8:264], in0=Xc[:, lo:lo + 256], in1=self.ctab[:, wi, 0, :], op=ALU.mult),
                         [nm("cb%d" % cc), nm("ctab")], [nm("cb%d" % (1 - cc))])
                self.eng(en, lambda e, Xc=Xc, lo=lo: e.tensor_tensor(out=hsel[:, 0:256], in0=Xc[:, lo + 1:lo + 257], in1=self.ctab[:, wi, 1, :], op=ALU.mult),
                         [nm("cb%d" % cc), nm("ctab")], [nm("hsel")])
                self.eng(en, lambda e, Xo=Xo: e.tensor_tensor(out=hsel[:, 0:256], in0=hsel[:, 0:256], in1=Xo[:, 8:264], op=ALU.add),
                         [nm("hsel"), nm("cb%d" % (1 - cc))], [nm("hsel")])
                self.eng(en, lambda e: e.tensor_tensor(out=pk[:, 0:NCTX], in0=hsel[:, 0:256], in1=xb[:, 0:NCTX], op=ALU.subtract),
                         [nm("hsel"), nm("xb")], [nm("pk")])
                for i in range(2):
                    self.eng(en, lambda e, i=i: e.memset(cbf[i][:], 0.0), [], [nm("cb%d" % i)])
            self.store(self.Pq[k * 128:(k + 1) * 128, :], pk[:], [nm("pk")], [("d", "Pq", "all")])

    def phase_c(self, l):
        nc, P = self.nc, self.P
        j = l // 2
        is_s5 = (l % 2 == 0)
        last = (l == 3)
        with_ctx = (l < 2)
        base = self.PERS
        nm = lambda s: "pc_%d_%s" % (l, s)
        o = base
        xt = [self.sb(nm("x%d" % i), [128, KC, 512], F32, o + i * 16384) for i in range(2)]; o += 32768
        zb = [self.sb(nm("zb%d" % i), [128, KC, 512], BF16, o + i * 8192) for i in range(2)]; o += 16384
        zf = self.sb(nm("zf"), [128, KC, 512], F32, o); o += 16384
        xn2 = self.sb(nm("xn2"), [128, KC, 512], BF16, o); o += 8192
        sq = self.sb(nm("sq"), [128, KC, 512], BF16, o); o += 8192
        a2 = self.sb(nm("a2"), [128, 32, 512], BF16, o); a2_off = o; o += 32768
        for f in range(32):
            P.region((nm("a2"), f), "sbuf", a2_off + f * 1024, a2_off + (f + 1) * 1024)
        w1p = [self.sb(nm("w1p%d" % i), [128, KC, 512], BF16, o + i * 8192) for i in range(2)]; o += 16384
        w2p = [self.sb(nm("w2p%d" % i), [128, 8, 512], BF16, o + i * 8192) for i in range(2)]; o += 16384
        mw = self.sb(nm("mw"), [128, KC, 1024], BF16, o); o += 16384
        tmp = self.sb(nm("tmp"), [128, 512], F32, o); o += 2048
        tmp2 = self.sb(nm("tmp2"), [128, 512], F32, o); o += 2048
        rs2 = self.sb(nm("rs2"), [128, 512], F32, o); o += 2048
        assert o <= SBUF_BYTES, o
        src = self.src_x(l)
        if is_s5:
            for h in range(2):
                self.load(zf[:], self.gluw[j][:, h * 512:(h + 1) * 512].rearrange("(k p) f -> p k f", p=128), [("g", "glu", j)], [nm("zf")])
                self.pool(lambda e, h=h: e.tensor_copy(out=mw[:, :, h * 512:(h + 1) * 512], in_=zf[:]), [nm("zf")], [nm("mw")])
        else:
            self.load(zf[:, :, 0:256], self.poolw[j].rearrange("g (kk p) c -> p (g kk) c", p=128), [("g", "pool", j)], [nm("zf")])
            self.pool(lambda e: e.tensor_copy(out=mw[:, :, 0:256], in_=zf[:, :, 0:256]), [nm("zf")], [nm("mw")])
        cbuf = [xt[0], xt[1]]
        cbn = [nm("x0"), nm("x1")]
        obuf = [zb[0], zb[1]]
        obn = [nm("zb0"), nm("zb1")]
        ci = 0
        for fg in range(8):
            b_, bn, ob, on = cbuf[ci % 2], cbn[ci % 2], obuf[ci % 2], obn[ci % 2]
            ci += 1
            self.load(b_[:], self.w1[l][:, fg * 512:(fg + 1) * 512].rearrange("(k p) f -> p k f", p=128), [("g", "w1", l)], [bn])
            self.pool(lambda e, b_=b_, ob=ob: e.tensor_copy(out=ob[:], in_=b_[:]), [bn], [on])
            self.store(self.w1b[l, :, fg * 512:(fg + 1) * 512].rearrange("(k p) f -> p k f", p=128), ob[:], [on], [("d", "w1b", l)])
        for dh in range(2):
            for fg in range(4):
                b_, bn, ob, on = cbuf[ci % 2], cbn[ci % 2], obuf[ci % 2], obn[ci % 2]
                ci += 1
                self.load(b_[:], self.w2[l][fg * 1024:(fg + 1) * 1024, dh * 512:(dh + 1) * 512].rearrange("(c p) f -> p c f", p=128), [("g", "w2", l)], [bn])
                self.pool(lambda e, b_=b_, ob=ob: e.tensor_copy(out=ob[:], in_=b_[:]), [bn], [on])
                self.store(self.w2b[l, fg * 1024:(fg + 1) * 1024, dh * 512:(dh + 1) * 512].rearrange("(c p) f -> p c f", p=128), ob[:], [on], [("d", "w2b", l)])
        w1i = w2i = 0
        for it, (c0, n, col) in enumerate(self.tiles(with_ctx)):
            x = xt[it % 2]
            xn_ = nm("x%d" % (it % 2))
            zt = zb[it % 2]
            zn = nm("zb%d" % (it % 2))
            G1 = lambda k: self.ab[:, l, 6 if not is_s5 else 4, k, col:col + 1]
            self.load(x[:, :, 0:n], src[:, c0:c0 + n].rearrange("(k p) t -> p k t", p=128), [("dx", c0)], [xn_])
            if is_s5:
                self.load(zf[:, :, 0:n], self.Z[:, c0:c0 + n].rearrange("(k p) t -> p k t", p=128), ["dz_all"], [nm("zf")])
                self.pool(lambda e, zt=zt, n=n: e.tensor_copy(out=zt[:, :, 0:n], in_=zf[:, :, 0:n]), [nm("zf")], [zn])
            else:
                self.load(zt[:, :, 0:n], self.Pq[:, c0:c0 + n].rearrange("(k p) t -> p k t", p=128), [("d", "Pq", "all")], [zn])
            for oc in range(KC):
                bank = oc % 2
                if is_s5:
                    for k in range(KC):
                        self.pe(lambda e, k=k, oc=oc, bank=bank, n=n, zt=zt: e.matmul(
                            self.ps[:, bank, 0:n], mw[:, k, oc * 128:(oc + 1) * 128], zt[:, k, 0:n],
                            start=(k == 0), stop=(k == KC - 1)), [nm("mw"), zn], [("ps", bank)])
                    gb = self.vecs[:, V_GLUB + 8 * j + oc:V_GLUB + 8 * j + oc + 1]
                    self.act(lambda e, bank=bank, n=n, gb=gb: e.activation(out=tmp[:, 0:n], in_=self.ps[:, bank, 0:n], func=AF.Sigmoid,
                                                                          bias=gb, scale=1.0), [("ps", bank), "vecs_sb"], [nm("tmp")])
                    self.dve(lambda e, n=n, oc=oc: e.tensor_tensor(out=tmp[:, 0:n], in0=tmp[:, 0:n], in1=zf[:, oc, 0:n], op=ALU.mult),
                             [nm("tmp"), nm("zf")], [nm("tmp")])
                    self.dve(lambda e, oc=oc, n=n, x=x: e.scalar_tensor_tensor(out=x[:, oc, 0:n], in0=tmp[:, 0:n], scalar=G1(oc),
                                                                              in1=x[:, oc, 0:n], op0=ALU.mult, op1=ALU.add),
                             [nm("tmp"), xn_, "ab"], [xn_])
                else:
                    gi = oc // 2
                    for kk in range(2):
                        self.pe(lambda e, kk=kk, gi=gi, oc=oc, bank=bank, n=n, zt=zt: e.matmul(
                            self.ps[:, bank, 0:n], mw[:, 2 * gi + kk, (oc % 2) * 128:(oc % 2) * 128 + 128], zt[:, 2 * gi + kk, 0:n],
                            start=(kk == 0), stop=(kk == 1)), [nm("mw"), zn], [("ps", bank)])
                    self.dve(lambda e, oc=oc, n=n, x=x, bank=bank: e.scalar_tensor_tensor(
                        out=x[:, oc, 0:n], in0=self.ps[:, bank, 0:n], scalar=G1(oc), in1=x[:, oc, 0:n], op0=ALU.mult, op1=ALU.add),
                        [("ps", bank), xn_, "ab"], [xn_])
            self.act(lambda e, x=x, n=n: e.activation(out=sq[:, :, 0:n], in_=x[:, :, 0:n], func=AF.Square), [xn_], [nm("sq")])
            for k in range(KC):
                self.pe(lambda e, k=k, n=n: e.matmul(self.ps[:, 6, 0:n], self.onesb[:], sq[:, k, 0:n], start=(k == 0), stop=(k == KC - 1)),
                        ["onesb", nm("sq")], [("ps", 6)])
            self.act(lambda e, n=n: e.activation(out=tmp[:, 0:n], in_=self.ps[:, 6, 0:n], func=AF.Sqrt, scale=1.0 / D, bias=self.epsb[:, 0:1]),
                     [("ps", 6), "epsb"], [nm("tmp")])
            self.dve(lambda e, n=n: e.reciprocal(out=rs2[:, 0:n], in_=tmp[:, 0:n]), [nm("tmp")], [nm("rs2")])
            for k in range(KC):
                self.dve(lambda e, k=k, n=n, x=x: e.tensor_tensor(out=tmp2[:, 0:n], in0=x[:, k, 0:n], in1=rs2[:, 0:n], op=ALU.mult),
                         [xn_, nm("rs2")], [nm("tmp2")])
                self.act(lambda e, k=k, n=n: e.activation(out=xn2[:, k, 0:n], in_=tmp2[:, 0:n], func=AF.Identity,
                                                         scale=self.ab[:, l, 2, k, col:col + 1], bias=self.ab[:, l, 3, k, col:col + 1]),
                         [nm("tmp2"), "ab"], [nm("xn2")])
            for fg in range(8):
                wp = w1p[w1i % 2]
                wpn = nm("w1p%d" % (w1i % 2))
                w1i += 1
                self.load(wp[:], self.w1b[l, :, fg * 512:(fg + 1) * 512].rearrange("(k p) f -> p k f", p=128), [("d", "w1b", l)], [wpn])
                for fc in range(4):
                    f = fg * 4 + fc
                    bank = f % 2
                    for k in range(KC):
                        self.pe(lambda e, k=k, fc=fc, bank=bank, n=n, wp=wp: e.matmul(
                            self.ps[:, bank, 0:n], wp[:, k, fc * 128:(fc + 1) * 128], xn2[:, k, 0:n],
                            start=(k == 0), stop=(k == KC - 1)), [wpn, nm("xn2")], [("ps", bank)])
                    b1 = self.vecs[:, V_B1 + 32 * l + f:V_B1 + 32 * l + f + 1]
                    tt, ttn = (tmp, nm("tmp")) if f % 2 == 0 else (tmp2, nm("tmp2"))
                    self.act(lambda e, bank=bank, n=n, b1=b1, tt=tt: e.activation(out=tt[:, 0:n], in_=self.ps[:, bank, 0:n], func=AF.Relu,
                                                                                 bias=b1, scale=1.0), [("ps", bank), "vecs_sb"], [ttn])
                    self.dve(lambda e, f=f, n=n, tt=tt: e.tensor_tensor(out=a2[:, f, 0:n], in0=tt[:, 0:n], in1=tt[:, 0:n], op=ALU.mult),
                             [ttn], [(nm("a2"), f)])
            for dh in range(2):
                for fg in range(4):
                    wp = w2p[w2i % 2]
                    wpn = nm("w2p%d" % (w2i % 2))
                    w2i += 1
                    self.load(wp[:], self.w2b[l, fg * 1024:(fg + 1) * 1024, dh * 512:(dh + 1) * 512].rearrange("(c p) f -> p c f", p=128),
                              [("d", "w2b", l)], [wpn])
                    for dc in range(4):
                        for fc in range(8):
                            f = fg * 8 + fc
                            self.pe(lambda e, dc=dc, fc=fc, f=f, n=n, wp=wp: e.matmul(
                                self.ps[:, 2 + dc, 0:n], wp[:, fc, dc * 128:(dc + 1) * 128], a2[:, f, 0:n],
                                start=(f == 0), stop=(f == 31), skip_group_check=True), [wpn, (nm("a2"), f)], [("ps", 2 + dc)])
                for dc in range(4):
                    oc = dh * 4 + dc
                    self.dve(lambda e, dc=dc, oc=oc, n=n, x=x: e.scalar_tensor_tensor(
                        out=x[:, oc, 0:n], in0=self.ps[:, 2 + dc, 0:n], scalar=self.ab[:, l, 5, oc, col:col + 1], in1=x[:, oc, 0:n],
                        op0=ALU.mult, op1=ALU.add), [("ps", 2 + dc), xn_, "ab"], [xn_])
                    self.act(lambda e, oc=oc, n=n, x=x: e.activation(out=x[:, oc, 0:n], in_=x[:, oc, 0:n], func=AF.Identity,
                                                                    bias=self.ab[:, l, 7, oc, col:col + 1], scale=1.0), [xn_, "ab"], [xn_])
            self.act(lambda e, x=x, n=n: e.activation(out=sq[:, :, 0:n], in_=x[:, :, 0:n], func=AF.Square), [xn_], [nm("sq")])
            for k in range(KC):
                self.pe(lambda e, k=k, n=n: e.matmul(self.ps[:, 6, 0:n], self.onesb[:], sq[:, k, 0:n], start=(k == 0), stop=(k == KC - 1)),
                        ["onesb", nm("sq")], [("ps", 6)])
            self.rstd_from_ps(6, n, c0, tmp, nm("tmp"))
            if not last:
                self.store(self.xs[:, c0:c0 + n].rearrange("(k p) t -> p k t", p=128), x[:, :, 0:n], [xn_], [("dx", c0)])
            else:
                for k in range(KC):
                    self.dve(lambda e, k=k, n=n, x=x: e.tensor_tensor(out=x[:, k, 0:n], in0=x[:, k, 0:n], in1=self.rstd[:, c0:c0 + n], op=ALU.mult),
                             [xn_, ("rstd", c0)], [xn_])
                    self.act(lambda e, k=k, n=n, x=x: e.activation(out=x[:, k, 0:n], in_=x[:, k, 0:n], func=AF.Identity,
                                                                  scale=self.vecs[:, V_FG + k:V_FG + k + 1]), [xn_, "vecs_sb"], [xn_])
                self.store(self.outT[:, c0 - NCTX:c0 - NCTX + n].rearrange("(k p) t -> p k t", p=128), x[:, :, 0:n], [xn_], [("d", "out", c0)])


def _chunk(v):
    v = np.asarray(v, np.float32).reshape(-1, 128)
    return np.ascontiguousarray(v.T)


def _bounds(n, w, mirrored):
    t = np.arange(n)
    h = w // 2
    if not mirrored:
        lo, hi = np.maximum(t - h, 0), np.minimum(t + h, n)
    else:
        lo, hi = np.maximum(t - h + 1, 0), np.minimum(t + h + 1, n)
    return (hi - lo).astype(np.float64)


def _host_inputs(inp, core):
    b, par = core // 2, core % 2
    f = np.float32
    x = inp["x"][b, par * NTOK:(par + 1) * NTOK]
    cx = inp["ctx"][b]
    if par:
        x, cx = x[::-1], cx[::-1]
    xT = np.ascontiguousarray(np.concatenate([cx, x], 0).T.astype(f))
    cT = np.stack([_chunk(inp["c"][b]), _chunk(inp["c_ctx"])], -1)
    vecs = np.zeros((128, NV), f)
    for l in range(4):
        vecs[:, V_N1G + 8 * l:V_N1G + 8 * l + 8] = _chunk(inp["norm1_g"][l])
        vecs[:, V_N2G + 8 * l:V_N2G + 8 * l + 8] = _chunk(inp["norm2_g"][l])
        vecs[:, V_ADAB + 48 * l:V_ADAB + 48 * l + 48] = _chunk(inp["ada_b"][l])
        vecs[:, V_B2 + 8 * l:V_B2 + 8 * l + 8] = _chunk(inp["mlp_b2"][l])
        vecs[:, V_B1 + 32 * l:V_B1 + 32 * l + 32] = _chunk(inp["mlp_b1"][l])
    for j in range(2):
        vecs[:, V_S5D + 8 * j:V_S5D + 8 * j + 8] = _chunk(inp["s5_d"][j])
        vecs[:, V_GLUB + 8 * j:V_GLUB + 8 * j + 8] = _chunk(inp["s5_glu_b"][j])
        vecs[:, V_PSC + 8 * j:V_PSC + 8 * j + 8] = _chunk(inp["pool_scale"][j])
    vecs[:, V_FG:V_FG + 8] = _chunk(inp["final_g"])
    s5p = np.zeros((2, 128, 3, 2, 32), f)
    s5B = np.zeros((2, 128, 2, 32, 2, 32), f)
    s5C = np.zeros((2, 128, 2, 32, 2, 32), f)
    for j in range(2):
        for d in range(2):
            dl = d if par == 0 else 1 - d
            for g2 in range(2):
                sl = slice(g2 * 64, g2 * 64 + 64)
                gs = np.arange(32) * 2 + g2
                s5p[j, sl, 0, d, :] = inp["s5_a_re"][j, dl][gs].T
                s5p[j, sl, 1, d, :] = inp["s5_a_im"][j, dl][gs].T
                s5p[j, sl, 2, d, :] = inp["s5_log_dt"][j, dl][gs][None, :]
                cs = slice(g2 * 16, g2 * 16 + 16)
                s5B[j, sl, d, :, 0, cs] = inp["s5_b_re"][j, dl][gs].transpose(1, 0, 2)
                s5B[j, sl, d, :, 1, cs] = inp["s5_b_im"][j, dl][gs].transpose(1, 0, 2)
                s5C[j, sl, d, :, 0, cs] = inp["s5_c_re"][j, dl][gs].transpose(2, 0, 1)
                s5C[j, sl, d, :, 1, cs] = inp["s5_c_im"][j, dl][gs].transpose(2, 0, 1)
    consts = np.zeros((128, 168), f)
    consts[:, 132:165] = np.arange(33, dtype=f)[None, :]
    consts[:, :128] = np.eye(128, dtype=f)
    consts[:, 128] = 1.0 if par == 0 else 0.0
    consts[:, 129] = 0.0 if par == 0 else 1.0
    m0, m1 = (1.0, 0.0) if par == 0 else (0.0, 1.0)
    ptab = np.zeros((128, 4, 4, 64), f)
    ctab = np.zeros((128, 4, 2, 256), f)
    for wi, w in enumerate(WINS):
        h = w // 2
        invc = 1.0 / _bounds(64, w, par == 1)
        r = np.arange(64)
        if par == 0:
            cntr = (r + h) - np.maximum(r - h, 0)
        else:
            cntr = (r + h + 1) - np.maximum(r - h + 1, 0)
        invr = 1.0 / cntr
        ptab[:, wi, 0, :] = m0 * invc
        ptab[:, wi, 1, :] = m1 * invc
        ptab[:, wi, 2, :] = m0 * invr
        ptab[:, wi, 3, :] = m1 * invr
        invs = 1.0 / _bounds(256, w, par == 1)
        ctab[:, wi, 0, :] = m0 * invs
        ctab[:, wi, 1, :] = m1 * invs
    return {
        "xT": xT, "cT": np.ascontiguousarray(cT.astype(f)), "vecs": vecs,
        "ada_w": np.ascontiguousarray(inp["ada_w"][:, core * 128:(core + 1) * 128]),
        "w1": np.ascontiguousarray(inp["mlp_w1"][:, core * 128:(core + 1) * 128]),
        "w2": np.ascontiguousarray(inp["mlp_w2"][:, core * 512:(core + 1) * 512]),
        "gluw": np.ascontiguousarray(inp["s5_glu_w"][:, core * 128:(core + 1) * 128]),
        "poolw": np.ascontiguousarray(inp["pool_w"].reshape(2, 1024, 256)[:, core * 128:(core + 1) * 128]),
        "s5p": s5p, "s5B": s5B, "s5C": s5C, "consts": consts, "ptab": ptab, "ctab": ctab,
    }


_BUILD = {}


def _get_builder(dbg=None, nlayers=4):
    key = (dbg, nlayers)
    if key not in _BUILD:
        _BUILD[key] = Builder(dbg=dbg, nlayers=nlayers)
    return _BUILD[key]


def kernel(**inputs):
    inp = {k: np.asarray(v, np.float32) for k, v in inputs.items()}
    bld = _get_builder()
    in_maps = [_host_inputs(inp, c) for c in range(8)]
    res = run_bass_kernel_spmd(bld.nc, in_maps, core_ids=list(range(8)))
    out = np.empty((4, 2 * NTOK, D), np.float32)
    for c in range(8):
        b, par = c // 2, c % 2
        o = res.results[c]["outT"].T
        if par:
            o = o[::-1]
        out[b, par * NTOK:(par + 1) * NTOK] = o
    return out
```
